# Optimizing a Trainium2 kernel written in Bass

```python
import jax, jax.numpy as jnp
from jax import lax
import numpy as np

D_MODEL = 4096
BATCH = 1
SEQ = 8192
DEPTH = 1

PLE_DIM = 256
HEAD_DIM = 64
ATTN_WIDTH = D_MODEL // 2
N_Q_HEADS = ATTN_WIDTH // HEAD_DIM
GQA_GROUP = 8
N_KV_HEADS = N_Q_HEADS // GQA_GROUP
KV_WIDTH = N_KV_HEADS * HEAD_DIM
WINDOW = 128
BLOCK = 128
POOL_WINDOWS = (2, 4, 8, 16)
N_POOL_GROUPS = 4
POOL_WIDTH = D_MODEL // 2
POOL_GROUP_DIM = POOL_WIDTH // N_POOL_GROUPS
IN_WIDTH = ATTN_WIDTH + 2 * KV_WIDTH + POOL_WIDTH + 2 * D_MODEL
D_FF = 11008
CONV_WIDTH = 3
RMS_EPS = 1e-6
MASK_VALUE = -1e30

kernel_name = "hybrid_swa_sink_pool_convffn_ple"


def rms_norm(x, gain):
    xf = x.astype(jnp.float32)
    y = xf * lax.rsqrt(jnp.mean(xf * xf, axis=-1, keepdims=True) + RMS_EPS)
    return (y * gain.astype(jnp.float32)).astype(x.dtype)


def sliding_window_attention(q, k, v, sinks):
    b, s, _ = q.shape
    nb = s // BLOCK
    qb = q.reshape(b, nb, BLOCK, N_KV_HEADS, GQA_GROUP, HEAD_DIM)

    def band(t):
        t = t.reshape(b, s, N_KV_HEADS, HEAD_DIM)
        t = jnp.pad(t, ((0, 0), (BLOCK, 0), (0, 0), (0, 0)))
        t = t.reshape(b, nb + 1, BLOCK, N_KV_HEADS, HEAD_DIM)
        return jnp.concatenate([t[:, :-1], t[:, 1:]], axis=2)

    kb, vb = band(k), band(v)
    scores = jnp.einsum('bnqhgd,bnkhd->bnhgqk', qb, kb).astype(jnp.float32) * (HEAD_DIM ** -0.5)
    qi = jnp.arange(BLOCK)[:, None]
    kj = jnp.arange(2 * BLOCK)[None, :]
    rel = kj - BLOCK - qi
    in_band = (rel <= 0) & (rel > -WINDOW)
    key_pos = jnp.arange(nb)[:, None, None] * BLOCK + kj[None] - BLOCK
    mask = in_band[None] & (key_pos >= 0)
    scores = jnp.where(mask[None, :, None, None], scores, MASK_VALUE)
    sink = jnp.broadcast_to(sinks.astype(jnp.float32).reshape(1, 1, N_KV_HEADS, GQA_GROUP, 1, 1),
                            scores.shape[:-1] + (1,))
    probs = jax.nn.softmax(jnp.concatenate([scores, sink], axis=-1), axis=-1)[..., :-1]
    out = jnp.einsum('bnhgqk,bnkhd->bnqhgd', probs.astype(v.dtype), vb)
    return out.reshape(b, s, ATTN_WIDTH)


def multiscale_pool(u, w_pool, pool_scale):
    b, s, _ = u.shape
    ug = u.reshape(b, s, N_POOL_GROUPS, POOL_GROUP_DIM)
    cs = jnp.cumsum(ug.astype(jnp.float32), axis=1)
    t = jnp.arange(1, s + 1, dtype=jnp.float32)
    means = []
    for gi, w in enumerate(POOL_WINDOWS):
        c = cs[:, :, gi]
        prev = jnp.pad(c, ((0, 0), (w, 0), (0, 0)))[:, :s]
        count = jnp.minimum(t, float(w))[None, :, None]
        means.append((c - prev) / count)
    mixed = (jnp.stack(means, axis=2) - ug.astype(jnp.float32)).astype(u.dtype)
    y = jnp.einsum('bsgc,gcd->bsgd', mixed, w_pool).reshape(b, s, POOL_WIDTH)
    return y * pool_scale


def causal_depthwise_conv(x, w, bias):
    y = lax.conv_general_dilated(
        x, w[:, None, :], window_strides=(1,), padding=((CONV_WIDTH - 1, 0),),
        dimension_numbers=('NWC', 'WIO', 'NWC'), feature_group_count=x.shape[-1])
    return y + bias


def setup_inputs(seed: int = 0) -> dict:
    key = jax.random.key(seed)
    ks = jax.random.split(key, 24)
    f32 = jnp.float32

    def w(k, shape, fan_in):
        return jax.random.normal(k, shape, f32) * (fan_in ** -0.5)

    def gain(k, n):
        return 1.0 + 0.05 * jax.random.normal(k, (DEPTH, n), f32)

    return {
        "x": jax.random.normal(ks[0], (BATCH, SEQ, D_MODEL), f32),
        "p": jax.random.normal(ks[1], (DEPTH, BATCH, SEQ, PLE_DIM), f32),
        "norm_mix_pre": gain(ks[2], D_MODEL),
        "w_in": w(ks[3], (DEPTH, D_MODEL, IN_WIDTH), D_MODEL),
        "attn_sinks": jax.random.normal(ks[4], (DEPTH, N_Q_HEADS), f32),
        "w_pool": w(ks[5], (DEPTH, N_POOL_GROUPS, POOL_GROUP_DIM, POOL_GROUP_DIM), POOL_GROUP_DIM),
        "pool_scale": 1.0 + 0.1 * jax.random.normal(ks[6], (DEPTH, POOL_WIDTH), f32),
        "w_branch_attn": w(ks[7], (DEPTH, ATTN_WIDTH, D_MODEL), ATTN_WIDTH),
        "w_branch_pool": w(ks[8], (DEPTH, POOL_WIDTH, D_MODEL), POOL_WIDTH),
        "w_out": w(ks[9], (DEPTH, D_MODEL, D_MODEL), D_MODEL),
        "norm_mix_post": gain(ks[10], D_MODEL),
        "norm_ffn_pre": gain(ks[11], D_MODEL),
        "w_up": w(ks[12], (DEPTH, D_MODEL, 2 * D_FF), D_MODEL),
        "conv_w": w(ks[13], (DEPTH, CONV_WIDTH, 2 * D_FF), CONV_WIDTH),
        "conv_b": 0.01 * jax.random.normal(ks[14], (DEPTH, 2 * D_FF), f32),
        "w_down": w(ks[15], (DEPTH, D_FF, D_MODEL), D_FF),
        "norm_ffn_post": gain(ks[16], D_MODEL),
        "norm_ple_gate": gain(ks[17], D_MODEL),
        "w_ple_gate": w(ks[18], (DEPTH, D_MODEL, D_MODEL), D_MODEL),
        "w_ple": w(ks[19], (DEPTH, PLE_DIM, D_MODEL), PLE_DIM),
        "norm_ple_post": gain(ks[20], D_MODEL),
    }


def reference(x, p, norm_mix_pre, w_in, attn_sinks, w_pool, pool_scale, w_branch_attn,
              w_branch_pool, w_out, norm_mix_post, norm_ffn_pre, w_up, conv_w, conv_b,
              w_down, norm_ffn_post, norm_ple_gate, w_ple_gate, w_ple, norm_ple_post):
    split_points = [ATTN_WIDTH, ATTN_WIDTH + KV_WIDTH, ATTN_WIDTH + 2 * KV_WIDTH,
                    ATTN_WIDTH + 2 * KV_WIDTH + POOL_WIDTH,
                    ATTN_WIDTH + 2 * KV_WIDTH + POOL_WIDTH + D_MODEL]
    for i in range(DEPTH):
        h = rms_norm(x, norm_mix_pre[i])
        z = h @ w_in[i]
        q, k, v, u, g_attn, g_pool = jnp.split(z, split_points, axis=-1)
        y_attn = sliding_window_attention(q, k, v, attn_sinks[i]) @ w_branch_attn[i]
        y_pool = multiscale_pool(u, w_pool[i], pool_scale[i]) @ w_branch_pool[i]
        merged = jax.nn.sigmoid(g_attn) * y_attn + jax.nn.sigmoid(g_pool) * y_pool
        x = x + rms_norm(merged @ w_out[i], norm_mix_post[i])
        h = rms_norm(x, norm_ffn_pre[i])
        up = causal_depthwise_conv(h @ w_up[i], conv_w[i], conv_b[i])
        gate, val = jnp.split(up, 2, axis=-1)
        x = x + rms_norm((jax.nn.gelu(gate, approximate=True) * val) @ w_down[i], norm_ffn_post[i])
        ple_gate = jax.nn.sigmoid(rms_norm(x, norm_ple_gate[i]) @ w_ple_gate[i])
        x = x + rms_norm(ple_gate * (p[i] @ w_ple[i]), norm_ple_post[i])
    return x
```

```python
import numpy as np
from contextlib import ExitStack
import concourse.bass as bass
import concourse.mybir as mybir
from concourse.bass_utils import run_bass_kernel_spmd

F32 = mybir.dt.float32
BF16 = mybir.dt.bfloat16
AF = mybir.ActivationFunctionType
ALU = mybir.AluOpType

D = 4096
NCH = 32
DFF = 11008
NFF = 86
TOK_CORE = 1024
TP = 512
NE = 514
HN = 257
NH = 768
COL_K, COL_V, COL_U, COL_GA, COL_GP = 2048, 2304, 2560, 4608, 8704
EPS = 1e-6
N_CORES = 8


class Buf:
    __slots__ = ("name", "w", "r")

    def __init__(self, name):
        self.name = name
        self.w = {}
        self.r = {}


def inherit(new_bufs, old_bufs):
    r = {}
    for b in old_bufs:
        for d in (b.w, b.r):
            for k, v in d.items():
                if r.get(k, 0) < v:
                    r[k] = v
    for nb in new_bufs:
        for k, v in r.items():
            if nb.r.get(k, 0) < v:
                nb.r[k] = v


class Sched:
    ENG = ("pe", "act", "dve", "pool", "sp")

    def __init__(self, nc):
        self.nc = nc
        self.ops = {k: [] for k in self.ENG}
        self.cnt = {}
        self.seen = {k: {} for k in self.ENG}
        self.dma_keys = set()

    def op(self, eng, meth, kw, reads=(), writes=(), dma=None, partial=False):
        fn = (meth, kw)
        deps = {}
        for b in reads:
            for k, v in b.w.items():
                if deps.get(k, 0) < v:
                    deps[k] = v
        for b in writes:
            for d in (b.w, b.r):
                for k, v in d.items():
                    if deps.get(k, 0) < v:
                        deps[k] = v
        waits = []
        seen = self.seen[eng]
        for k, v in deps.items():
            if k == eng and eng == "pe":
                continue
            if seen.get(k, 0) < v:
                seen[k] = v
                waits.append((k, v))
        if dma is not None:
            key = dma
            self.dma_keys.add(key)
        else:
            key = eng
        tid = self.cnt.get(key, 0) + 1
        self.cnt[key] = tid
        self.ops[eng].append((fn, waits, key, tid))
        for b in reads:
            if b.r.get(key, 0) < tid:
                b.r[key] = tid
        for b in writes:
            if partial:
                b.w[key] = tid
            else:
                b.w = {key: tid}
                b.r = {}

    def emit(self, final_waits=()):
        nc = self.nc
        needed = {}
        for eng in self.ENG:
            for fn, waits, key, tid in self.ops[eng]:
                for k, v in waits:
                    needed.setdefault(k, set()).add(v)
        fin = []
        for eng, b in final_waits:
            for k, v in b.w.items():
                needed.setdefault(k, set()).add(v)
                fin.append((eng, k, v))
        renum = {}
        for k, s in needed.items():
            if k in self.dma_keys:
                continue
            renum[k] = {v: i + 1 for i, v in enumerate(sorted(s))}
        keys = sorted(set(list(needed.keys()) + list(self.dma_keys)))
        with ExitStack() as st:
            sems = {k: st.enter_context(nc.semaphore("s_" + k)) for k in keys}
            block = st.enter_context(nc.Block())

            def val(k, v):
                return 16 * v if k in self.dma_keys else renum[k][v]

            def mk(eng):
                def body(engine):
                    for fn, waits, key, tid in self.ops[eng]:
                        for k, v in waits:
                            engine.wait_ge(sems[k], val(k, v))
                        inst = getattr(engine, fn[0])(**fn[1])
                        if key in self.dma_keys:
                            inst.then_inc(sems[key], 16)
                        elif key in renum and tid in renum[key]:
                            inst.then_inc(sems[key], 1)
                    for e2, k, v in fin:
                        if e2 == eng:
                            engine.wait_ge(sems[k], val(k, v))
                return body

            block.tensor(mk("pe"))
            block.scalar(mk("act"))
            block.vector(mk("dve"))
            block.gpsimd(mk("pool"))
            block.sync(mk("sp"))


ARENA_BYTES = 164480
STG_ELEMS = 1024
KBLK = 2
N_STG = 4
N_BF = 4


class _Stop(Exception):
    pass


def build_nc(debug=None, n_pass=2, stop_after=None, lite=False):
    debug = debug or ()
    nc = bass.Bass("TRN2", target_bir_lowering=False)
    S = Sched(nc)

    def O(eng, meth, reads, writes, partial=False, dma=None, **kw):
        S.op(eng, meth, kw, reads=reads, writes=writes, dma=dma, partial=partial)

    def din(name, shape, dt=F32):
        if lite and name.startswith("w_"):
            return None
        return nc.dram_tensor(name, list(shape), dt, kind="ExternalInput").ap()

    xh = din("xh", [TOK_CORE + 256, D])
    p_in = din("p", [TOK_CORE, 256])
    w_in = din("w_in", [D, 12800])
    w_pool = din("w_pool", [4, 512, 512])
    w_ba = din("w_ba", [2048, D])
    w_bp = din("w_bp", [2048, D])
    w_out = din("w_out", [D, D])
    w_up = din("w_up", [D, 2 * DFF])
    w_down = din("w_down", [DFF, D])
    w_pg = din("w_pg", [D, D])
    w_ple = din("w_ple", [256, D])
    gains_d = din("gains", [128, 6 * NCH])
    conv_d = din("convp", [128, 172 * 4])
    pscale_d = din("pscale", [128, 16])
    sinks_d = din("sinks", [128, 16])
    invc_d = din("invc", [128, 2 * 4 * NE])
    hv_d = din("hv", [128, 2])
    mask_d = din("mask", [128, 1032], BF16)
    ident_d = din("ident", [128, 128])
    g0b_d = din("g0b", [128, D])
    out_d = nc.dram_tensor("out", [TOK_CORE, D], F32, kind="ExternalOutput").ap()
    scr_d = nc.dram_tensor("scr", [128, NCH * NE], F32, kind="Internal").ap()
    dbg_out = {}

    st = ExitStack()
    with st:
        def sb(name, shape, dt):
            return st.enter_context(nc.sbuf_tensor("sb_" + name, list(shape), dt))

        arena = sb("arena", [128, ARENA_BYTES // 2], BF16)
        stg = [sb(f"stg{i}", [128, STG_ELEMS], F32) for i in range(N_STG)]
        stgB = [Buf(f"stg{i}") for i in range(N_STG)]
        wbf = [sb(f"wbf{i}", [128, STG_ELEMS], BF16) for i in range(N_BF)]
        wbfB = [Buf(f"wbf{i}") for i in range(N_BF)]
        gains = sb("gains", [128, 6, NCH], F32)
        convp = sb("convp", [128, 172, 4], F32)
        pscale = sb("pscale", [128, 16], F32)
        esink = sb("esink", [128, 16], F32)
        hv = sb("hv", [128, 2], F32)
        ident = sb("ident", [128, 128], F32)
        maskp = sb("maskp", [128, 1032], BF16)
        ones_s = sb("ones_s", [128, 2, 128], BF16)
        ones_f = sb("ones_f", [128, 128], BF16)
        rstd_b = sb("rstd_b", [128, NE], F32)
        sqt = [sb(f"sqt{i}", [128, NE], BF16) for i in range(2)]
        stat = sb("stat", [128, 16], F32)
        pstage = sb("pstage", [128, 256], F32)
        pT = sb("pT", [128, 2, TP], BF16)
        sgple = sb("sgple", [128, 4, TP], F32)
        constB = Buf("const")
        rstdB = Buf("rstd_b")
        sqtB = [Buf("sqt0"), Buf("sqt1")]
        statB = Buf("stat")
        pstageB = Buf("pstage")
        pTB = Buf("pT")
        maskpB = Buf("maskp")
        sgpleB = Buf("sgple")

        acc = [st.enter_context(nc.psum_tensor(f"acc{i}", [128, 2, 512], F32)) for i in range(4)]
        accB = [Buf(f"acc{i}") for i in range(4)]
        bankB = [Buf(f"bank{i}") for i in range(8)]
        acc_rr = [0]

        acc_reserved = set()

        def next_acc():
            while True:
                i = acc_rr[0]
                acc_rr[0] = (i + 1) % 4
                if i not in acc_reserved:
                    return i

        def abytes(off, nbytes, dt, pattern=None, **kw):
            assert off % 4 == 0 and nbytes % 4 == 0 and off + nbytes <= ARENA_BYTES, (off, nbytes)
            v = arena[:, off // 2:(off + nbytes) // 2]
            if dt == F32:
                v = v.bitcast(F32)
            if pattern:
                v = v.rearrange(pattern, **kw)
            return v

        SZ_H0 = NCH * NH * 2
        SZ_KP = 8 * NH * 2
        SZ_VP = 6 * 8 * 128 * 2
        SZ_AT = 16 * NE * 2
        SZ_MG = NCH * NE * 2
        SZ_R = NCH * NE * 4
        SZ_A = NFF * TP * 2
        O_H0 = 0
        O_KP = O_H0 + SZ_H0
        O_VP = O_KP + SZ_KP
        O_AT = O_VP + SZ_VP
        O_PO = O_AT + SZ_AT
        O_MG = O_PO + SZ_AT
        O_SP = O_MG + SZ_MG
        O_R2 = ARENA_BYTES - SZ_R

        h0 = abytes(O_H0, SZ_H0, BF16, "p (c n) -> p c n", c=NCH)
        kpad = abytes(O_KP, SZ_KP, BF16, "p (j n) -> p j n", j=8)
        vpad = abytes(O_VP, SZ_VP, BF16, "p (t j s d) -> p t j s d", t=6, j=4, s=2)
        vpad_flat = abytes(O_VP, SZ_VP, BF16)
        attn = abytes(O_AT, SZ_AT, BF16, "p (c n) -> p c n", c=16)
        pool_o = abytes(O_PO, SZ_AT, BF16, "p (c n) -> p c n", c=16)
        merged = abytes(O_MG, SZ_MG, BF16, "p (c n) -> p c n", c=NCH)
        R1 = abytes(0, SZ_R, F32, "p (c n) -> p c n", c=NCH)
        R1f = abytes(0, SZ_R, F32)
        R2 = abytes(O_R2, SZ_R, F32, "p (c n) -> p c n", c=NCH)
        R3 = abytes(SZ_MG, SZ_R, F32, "p (c n) -> p c n", c=NCH)
        a_t = abytes(0, SZ_A, BF16, "p (c n) -> p c n", c=NFF)
        h1 = abytes(O_R2, SZ_MG, BF16, "p (c n) -> p c n", c=NCH)
        h2 = abytes(0, SZ_MG, BF16, "p (c n) -> p c n", c=NCH)
        ostage = [abytes(0, 16384, F32), abytes(16384, 16384, F32)]
        ostB2 = [Buf("ost0"), Buf("ost1")]
        xs_m = abytes(O_SP, 16384, F32)
        xs_m2 = abytes(O_MG, 16384, F32)
        xs_m3 = abytes(O_KP, 16384, F32)
        kvtmpB = Buf("kvtmp")
        g0b = abytes(O_AT + 8192, 16384, F32)
        invc = abytes(O_SP + 16384, 4 * NE * 4, F32, "p (g n) -> p g n", g=4)
        invc_flat = abytes(O_SP + 16384, 4 * NE * 4, F32)
        q_tmp = abytes(O_SP, 4 * NE * 2, BF16, "p (c n) -> p c n", c=4)
        E_all = abytes(O_SP + 4112, 4 * 1032 * 2, BF16, "p (a s n) -> p a s n", a=2, s=2)
        rt_t = abytes(O_SP + 12368, NE * 4, F32)
        u_t = abytes(O_MG + 11288, 640 * 4, F32)
        sga = abytes(O_SP, 4 * NE * 4, F32, "p (c n) -> p c n", c=4)
        sgp = abytes(O_SP + 8224, 4 * NE * 4, F32, "p (c n) -> p c n", c=4)
        wk = abytes(O_MG, NCH * 192 * 2, BF16, "p (k n) -> p k n", k=NCH)
        wk_flat = abytes(O_MG, NCH * 192 * 2, BF16)
        wv = abytes(O_MG + 12288, NCH * 256 * 2, BF16, "p (k n) -> p k n", k=NCH)
        sqjunk = abytes(O_AT, 8192, BF16)
        s_t = [abytes(O_MG + 0, 640 * 4, F32), abytes(O_MG + 2560, 640 * 4, F32)]
        mixed2 = [abytes(O_MG + 5120, 4 * NE * 2, BF16, "p (c n) -> p c n", c=4),
                  abytes(O_MG + 13848, 4 * NE * 2, BF16, "p (c n) -> p c n", c=4)]
        mixB = [Buf("mix0"), Buf("mix1")]
        tmp_p = abytes(O_MG + 5120 + 4112, NE * 4, F32)
        xs_r = [abytes(SZ_R + 192, 16384, F32), abytes(SZ_R + 192 + 16384, 16384, F32)]
        xsrB = [Buf("xsr0"), Buf("xsr1")]
        O_F = O_R2 + SZ_MG
        upc = [abytes(O_F + 12288 + i * 2064, 2 * 258 * 4, F32, "p (h n) -> p h n", h=2) for i in range(4)]
        upcB = [Buf(f"upc{i}") for i in range(4)]
        cvg = [abytes(O_F + i * 2048, TP * 4, F32) for i in range(4)]
        cvv = [abytes(O_F + 8192 + i * 2048, TP * 4, F32) for i in range(2)]
        gact = abytes(O_F + 20544, 4 * TP * 2, BF16, "p (c n) -> p c n", c=4)
        xr_t = abytes(SZ_A, 4 * NE * 4, F32, "p (c n) -> p c n", c=4)
        xr_flat = abytes(SZ_A, 4 * NE * 4, F32)

        h0B = [Buf(f"h0_{c}") for c in range(NCH)]
        kpB = [Buf(f"kp{j}") for j in range(8)]
        vpB = [Buf(f"vp{t}") for t in range(6)]
        atB = [Buf(f"at{c}") for c in range(16)]
        poB = [Buf(f"po{c}") for c in range(16)]
        mgB = [Buf(f"mg{c}") for c in range(NCH)]
        R1B = [Buf(f"R1_{c}") for c in range(NCH)]
        R2B = [Buf(f"R2_{c}") for c in range(NCH)]
        R3B = [Buf(f"R3_{c}") for c in range(NCH)]
        aB = [Buf(f"a{c}") for c in range(NFF)]
        h1B = [Buf(f"h1_{c}") for c in range(NCH)]
        h2B = [Buf(f"h2_{c}") for c in range(NCH)]
        spB = Buf("spare")
        mgtmpB = Buf("mgtmp")
        ffB = Buf("fftmp")
        xrB = Buf("xr")
        ostB = Buf("ostage")
        scrB = Buf("scr")
        outB = Buf("out")
        dbgB = Buf("dbg")
        qB = [Buf(f"q{i}") for i in range(4)]
        EB = [Buf(f"E{i}") for i in range(4)]
        rtB = Buf("rt")
        uB = Buf("u")
        sgaB = [Buf(f"sga{i}") for i in range(4)]
        sgpB = [Buf(f"sgp{i}") for i in range(4)]
        cvgB = [Buf(f"cvg{i}") for i in range(4)]
        cvvB = [Buf(f"cvv{i}") for i in range(2)]
        gactB = [Buf(f"gact{i}") for i in range(4)]
        TMPB = qB + EB + [rtB, uB] + sgaB + sgpB + cvgB + cvvB + gactB + mixB
        ALLB = (h0B + kpB + vpB + atB + poB + mgB + R1B + R2B + R3B + aB + h1B + h2B +
                [spB, mgtmpB, ffB, xrB, ostB, kvtmpB] + TMPB + xsrB + ostB2 + upcB)

        def dbg(name, ap, shape, dt, bufs):
            flush()
            if name not in debug:
                if stop_after == name:
                    raise _Stop()
                return
            t = nc.dram_tensor("dbg_" + name, list(shape), dt, kind="ExternalOutput").ap()
            dbg_out[name] = True
            O("sp", "dma_start", bufs, [dbgB], partial=True, dma="dbg", out=t, in_=ap)
            if stop_after == name:
                raise _Stop()

        def ld(dst, src):
            O("sp", "dma_start", [], [constB], partial=True, dma="cst", out=dst, in_=src)

        ld(gains[:].rearrange("p a b -> p (a b)"), gains_d)
        ld(convp[:].rearrange("p a b -> p (a b)"), conv_d)
        ld(pscale[:], pscale_d)
        ld(esink[:], sinks_d)
        ld(hv[:], hv_d)
        ld(ident[:], ident_d)
        O("act", "activation", [constB], [constB], out=esink[:], in_=esink[:], func=AF.Exp)
        O("dve", "memset", [], [constB], partial=True, ap=ones_f[:], constant=1.0)
        O("dve", "memset", [], [constB], partial=True, ap=ones_s[:].rearrange("p a b -> p (a b)"), constant=0.0)
        O("dve", "memset", [constB], [constB], ap=ones_s[:, 0, 0:64], constant=1.0)
        O("dve", "memset", [constB], [constB], ap=ones_s[:, 1, 64:128], constant=1.0)

        def chk(name):
            flush()
            if stop_after == name:
                raise _Stop()

        ws_i = [0, 0, 0]
        cast_pol = [["act", "dve"]]

        def cast_op(eng, dst, src, rbufs, wbufs, partial=False):
            if eng == "act":
                O("act", "activation", rbufs, wbufs, partial=partial, out=dst, in_=src, func=AF.Copy)
            else:
                O(eng, "tensor_copy", rbufs, wbufs, partial=partial, out=dst, in_=src)

        def next_stg():
            si = ws_i[0] % N_STG
            ws_i[0] += 1
            return si

        def load_slab(src_ap, nk, ncols):
            assert nk * ncols <= STG_ELEMS
            si = next_stg()
            bi = ws_i[1] % N_BF
            ws_i[1] += 1
            sv = stg[si][:, 0:nk * ncols].rearrange("p (k n) -> p k n", k=nk)
            bv = wbf[bi][:, 0:nk * ncols].rearrange("p (k n) -> p k n", k=nk)
            O("sp", "dma_start", [], [stgB[si]], dma=f"stg{si}", out=sv,
              in_=src_ap.rearrange("(k p) f -> p k f", p=128))
            ce = cast_pol[0][ws_i[2] % len(cast_pol[0])]
            ws_i[2] += 1
            cast_op(ce, bv, sv, [stgB[si]], [wbfB[bi]])
            return bv, wbfB[bi]

        JQ = []
        LOOKAHEAD = 3

        def proj_fm(wsrc, K, col0, n_f, rhs_fn, rhs_bufs_fn, halves, evac_fn):
            for g0 in range(0, n_f, 4):
                ng = min(4, n_f - g0)
                state = {}
                kb_list = list(range(0, K, KBLK))
                for kb in kb_list:
                    nk = min(KBLK, K - kb)
                    src = wsrc[kb * 128:(kb + nk) * 128, col0 + g0 * 128:col0 + (g0 + ng) * 128]

                    def run(slab, slabB, g0=g0, ng=ng, kb=kb, nk=nk, state=state, last=(kb == kb_list[-1])):
                        if kb == 0:
                            state["accs"] = [next_acc() for _ in range(ng)]
                        accs = state["accs"]
                        for fi in range(ng):
                            ai = accs[fi]
                            for k in range(nk):
                                kk = kb + k
                                for h, n in enumerate(halves):
                                    O("pe", "matmul", [slabB] + rhs_bufs_fn(kk), [accB[ai]], partial=True,
                                      out=acc[ai][:, h, 0:n], lhsT=slab[:, k, fi * 128:(fi + 1) * 128],
                                      rhs=rhs_fn(kk, h), start=(kk == 0), stop=(kk == K - 1))
                            if last:
                                post = evac_fn(g0 + fi, fi, ai)
                                if post is not None:
                                    state.setdefault("post", []).append(post)
                        if last:
                            for post in state.get("post", []):
                                post()

                    JQ.append((src, nk, ng * 128, run))

        def code(fn):
            JQ.append((None, 0, 0, fn))

        deferred = []
        cur_job = [0]

        def defer(fn, delay):
            deferred.append((cur_job[0] + delay, fn))

        def run_deferred(upto):
            keep = []
            for due, fn in deferred:
                if due <= upto:
                    fn()
                else:
                    keep.append((due, fn))
            deferred[:] = keep

        def flush():
            loaded = {}
            nxt = 0
            n = len(JQ)
            for i in range(n):
                cur_job[0] = i
                run_deferred(i)
                while nxt < n and nxt <= i + LOOKAHEAD:
                    jb = JQ[nxt]
                    if jb[0] is not None:
                        loaded[nxt] = load_slab(jb[0], jb[1], jb[2])
                    nxt += 1
                jb = JQ[i]
                if jb[0] is None:
                    jb[3]()
                else:
                    jb[3](*loaded.pop(i))
            run_deferred(10 ** 9)
            del JQ[:]

        def split2(ap):
            return ap.rearrange("p (h n) -> p h n", h=2)

        def fm_rstd(Rt, RB):
            ai = next_acc()
            for c in range(NCH):
                q = c % 2
                O("act", "activation", [RB[c]], [sqtB[q]], out=sqt[q][:], in_=Rt[:, c, :], func=AF.Square)
                for h in range(2):
                    O("pe", "matmul", [sqtB[q], constB], [accB[ai]], partial=True,
                      out=acc[ai][:, h, 0:HN], lhsT=ones_f[:], rhs=sqt[q][:, h * HN:(h + 1) * HN],
                      start=(c == 0), stop=(c == NCH - 1))
            O("dve", "tensor_scalar", [accB[ai]], [rstdB], out=split2(rstd_b[:]), in0=acc[ai][:, 0:2, 0:HN],
              scalar1=1.0 / D, scalar2=EPS, op0=ALU.mult, op1=ALU.add)
            O("act", "activation", [rstdB], [rstdB], out=rstd_b[:], in_=rstd_b[:], func=AF.Sqrt)
            O("dve", "reciprocal", [rstdB], [rstdB], out=rstd_b[:], in_=rstd_b[:])

        def sq_add(Rt, RB, c, ai, first, last):
            q = c % 2
            O("act", "activation", [RB[c]], [sqtB[q]], out=sqt[q][:], in_=Rt[:, c, :], func=AF.Square)
            for h in range(2):
                O("pe", "matmul", [sqtB[q], constB], [accB[ai]], partial=True,
                  out=acc[ai][:, h, 0:HN], lhsT=ones_f[:], rhs=sqt[q][:, h * HN:(h + 1) * HN],
                  start=first, stop=last)

        def rstd_finish(ai):
            O("dve", "tensor_scalar", [accB[ai]], [rstdB], out=split2(rstd_b[:]), in0=acc[ai][:, 0:2, 0:HN],
              scalar1=1.0 / D, scalar2=EPS, op0=ALU.mult, op1=ALU.add)
            O("act", "activation", [rstdB], [rstdB], out=rstd_b[:], in_=rstd_b[:], func=AF.Sqrt)
            O("dve", "reciprocal", [rstdB], [rstdB], out=rstd_b[:], in_=rstd_b[:])

        def h_scale(Rt, RB, gidx, ht, hB):
            for c in range(NCH):
                O("dve", "scalar_tensor_tensor", [rstdB, constB, RB[c]], [hB[c]], out=ht[:, c, :], in0=Rt[:, c, :],
                  scalar=gains[:, gidx, c:c + 1], in1=rstd_b[:], op0=ALU.mult, op1=ALU.mult)

        def post_scale(Rt, RB, gidx):
            fm_rstd(Rt, RB)
            for c in range(NCH):
                O("dve", "scalar_tensor_tensor", [rstdB, constB], [RB[c]], out=Rt[:, c, :], in0=Rt[:, c, :],
                  scalar=gains[:, gidx, c:c + 1], in1=rstd_b[:], op0=ALU.mult, op1=ALU.mult)

        def pre_norm(Rt, RB, gidx, ht, hB):
            fm_rstd(Rt, RB)
            for c in range(NCH):
                O("dve", "scalar_tensor_tensor", [rstdB, constB, RB[c]], [hB[c]], out=ht[:, c, :], in0=Rt[:, c, :],
                  scalar=gains[:, gidx, c:c + 1], in1=rstd_b[:], op0=ALU.mult, op1=ALU.mult)

        def x_rows(ps, t):
            r0 = ps * TP + t * 128
            return xh[r0:r0 + 128, :]

        def rhs_h0e(k, h):
            return h0[:, k, 254 + h * HN:254 + (h + 1) * HN]

        def evac_copy(Rt, RB, full):
            def f_(f, fi, ai):
                if full:
                    O("act", "activation", [accB[ai]], [RB[f]], out=split2(Rt[:, f, :]),
                      in_=acc[ai][:, 0:2, 0:HN], func=AF.Copy)
                else:
                    O("act", "activation", [accB[ai]], [RB[f]], partial=True, out=split2(Rt[:, f, 2:NE]),
                      in_=acc[ai][:, 0:2, 0:256], func=AF.Copy)
            return f_

        def run_pass(ps):
            inherit(ALLB, ALLB)
            O("sp", "dma_start", [], [spB], partial=True, dma="cst2", out=invc_flat,
              in_=invc_d[:, ps * 4 * NE:(ps + 1) * 4 * NE])
            O("sp", "dma_start", [], [maskpB], dma="cst3", out=maskp[:], in_=mask_d)
            O("dve", "tensor_scalar", [constB, maskpB], [maskpB], out=maskp[:, 0:128], in0=maskp[:, 0:128],
              scalar1=hv[:, ps:ps + 1], scalar2=None, op0=ALU.mult)
            O("dve", "tensor_scalar", [constB, maskpB], [maskpB], out=maskp[:, 1024:1032], in0=maskp[:, 1024:1032],
              scalar1=hv[:, ps:ps + 1], scalar2=None, op0=ALU.mult)

            chk("const")
            xsl = [xs_m, xs_m2, xs_m3]
            xsB = [spB, mgtmpB, kvtmpB]
            inherit([kvtmpB], kpB + vpB)
            O("sp", "dma_start", [], [atB[1]], dma="cst4", out=g0b, in_=g0b_d)
            for t in range(6):
                xq = xsl[t % 3]
                xb = xsB[t % 3]
                O("sp", "dma_start", [], [xb], dma=f"xs{t % 3}", out=xq, in_=x_rows(ps, t))
                O("act", "activation", [xb], [statB, atB[0]], partial=True, out=sqjunk, in_=xq, func=AF.Square,
                  accum_out=stat[:, t:t + 1])
                O("dve", "tensor_scalar", [statB], [statB], out=stat[:, 8 + t:9 + t], in0=stat[:, t:t + 1],
                  scalar1=1.0 / D, scalar2=EPS, op0=ALU.mult, op1=ALU.add)
                O("act", "activation", [statB], [statB], out=stat[:, 8 + t:9 + t], in_=stat[:, 8 + t:9 + t],
                  func=AF.Sqrt)
                O("dve", "reciprocal", [statB], [statB], out=stat[:, 8 + t:9 + t], in_=stat[:, 8 + t:9 + t])
                O("dve", "scalar_tensor_tensor", [statB, xb, atB[1]], [xb], out=xq, in0=xq,
                  scalar=stat[:, 8 + t:9 + t], in1=g0b, op0=ALU.mult, op1=ALU.mult)
                for c0 in range(0, NCH, 8):
                    ai = next_acc()
                    for cc in range(8):
                        c = c0 + cc
                        O("pe", "transpose", [xb, constB], [accB[ai]], partial=True,
                          out=acc[ai][:, cc // 4, (cc % 4) * 128:(cc % 4 + 1) * 128],
                          in_=xq[:, c * 128:(c + 1) * 128], identity=ident[:])
                    for hb in range(2):
                        cb = c0 + hb * 4
                        src = acc[ai][:, hb, :].rearrange("p (c n) -> p c n", c=4)
                        dst = h0[:, cb:cb + 4, t * 128:(t + 1) * 128]
                        if (c0 // 8) % 2 == 0:
                            O("act", "activation", [accB[ai]], h0B[cb:cb + 4], partial=True, out=dst, in_=src,
                              func=AF.Copy)
                        else:
                            O("dve", "tensor_copy", [accB[ai]], h0B[cb:cb + 4], partial=True, out=dst, in_=src)
            dbg("h0", h0[:].rearrange("p c n -> p (c n)"), [128, NCH * NH], BF16, h0B)

            inherit(kpB + vpB, [kvtmpB])
            O("pool", "memset", [], [mgtmpB], ap=wk_flat, constant=0.0)
            for j in range(4):
                for hf in range(2):
                    si = next_stg()
                    sv = stg[si][:, 0:16 * 64].rearrange("p (k n) -> p k n", k=16)
                    for q2 in range(2):
                        r0 = hf * 2048 + q2 * 1024
                        O("sp", "dma_start", [], [stgB[si]], partial=(q2 > 0), dma=f"stg{si}",
                          out=sv[:, q2 * 8:(q2 + 1) * 8, :],
                          in_=w_in[r0:r0 + 1024, COL_K + j * 64:COL_K + (j + 1) * 64].rearrange(
                              "(k p) f -> p k f", p=128))
                    cast_op("dve" if hf == 0 else "act", wk[:, hf * 16:(hf + 1) * 16, 64:128], sv, [stgB[si]],
                            [mgtmpB], partial=True)
                for s in range(2):
                    ai = next_acc()
                    cc0 = 64 if s == 0 else 0
                    for k in range(NCH):
                        for h in range(2):
                            O("pe", "matmul", [mgtmpB, h0B[k]], [accB[ai]], partial=True,
                              out=acc[ai][:, h, 0:384], lhsT=wk[:, k, cc0:cc0 + 128],
                              rhs=h0[:, k, h * 384:(h + 1) * 384], start=(k == 0), stop=(k == NCH - 1))
                    O("act", "activation", [accB[ai]], [kpB[j * 2 + s]], out=split2(kpad[:, j * 2 + s, :]),
                      in_=acc[ai][:, 0:2, 0:384], func=AF.Copy)
            dbg("kpad", kpad[:].rearrange("p j n -> p (j n)"), [128, 8 * NH], BF16, kpB)

            for q8 in range(8):
                si = next_stg()
                sv = stg[si][:, 0:4 * 256].rearrange("p (k n) -> p k n", k=4)
                O("sp", "dma_start", [], [stgB[si]], dma=f"stg{si}", out=sv,
                  in_=w_in[q8 * 512:(q8 + 1) * 512, COL_V:COL_V + 256].rearrange("(k p) f -> p k f", p=128))
                cast_op("dve" if q8 % 2 == 0 else "act", wv[:, q8 * 4:(q8 + 1) * 4, :], sv, [stgB[si]], [mgtmpB],
                        partial=True)
            O("pool", "memset", [], vpB, ap=vpad_flat, constant=0.0)
            for t2 in range(3):
                ai = next_acc()
                for tt in range(2):
                    t = t2 * 2 + tt
                    for k in range(NCH):
                        O("pe", "matmul", [mgtmpB, h0B[k]], [accB[ai]], partial=True,
                          out=acc[ai][:, tt, 0:256], lhsT=h0[:, k, t * 128:(t + 1) * 128], rhs=wv[:, k, :],
                          start=(k == 0), stop=(k == NCH - 1))
                    src = acc[ai][:, tt, 0:256].rearrange("p (j d) -> p j d", j=4)
                    if t % 2 == 0:
                        O("act", "activation", [accB[ai]], [vpB[t]], partial=True, out=vpad[:, t, :, 0, 0:64],
                          in_=src, func=AF.Copy)
                        O("act", "activation", [accB[ai]], [vpB[t]], partial=True, out=vpad[:, t, :, 1, 64:128],
                          in_=src, func=AF.Copy)
                    else:
                        O("dve", "tensor_copy", [accB[ai]], [vpB[t]], partial=True, out=vpad[:, t, :, 0, 0:64],
                          in_=src)
                        O("dve", "tensor_copy", [accB[ai]], [vpB[t]], partial=True, out=vpad[:, t, :, 1, 64:128],
                          in_=src)
            dbg("vpad", vpad_flat, [128, 6 * 8 * 128], BF16, vpB)

            inherit(qB + EB + [rtB], [spB])
            for g in range(4):
                j = g

                def evac_q(f, fi, ai):
                    O("act", "activation", [accB[ai]], [qB[fi]], out=split2(q_tmp[:, fi, :]),
                      in_=acc[ai][:, 0:2, 0:HN], func=AF.Copy)

                proj_fm(w_in, NCH, g * 512, 4, rhs_h0e, lambda k: [h0B[k]], [HN, HN], evac_q)

                def att_S(fi, j=j):
                    par = fi % 2
                    for s_ in range(2):
                        kp = kpad[:, j * 2 + s_, :]
                        for b in range(4):
                            for w in range(2):
                                kt = b + 1 + w
                                reg = (b * 2 + w) * 128
                                bk = 2 * s_ + reg // 512
                                O("pe", "matmul", [kpB[j * 2 + s_], qB[fi]], [bankB[bk]], partial=True,
                                  out=acc[s_][:, reg // 512, reg % 512:reg % 512 + 128],
                                  lhsT=kp[:, kt * 128:(kt + 1) * 128],
                                  rhs=q_tmp[:, fi, 2 + b * 128:2 + (b + 1) * 128], start=True, stop=True)
                        for w in range(2):
                            o0 = s_ * 8 + w * 2
                            O("pe", "matmul", [kpB[j * 2 + s_], qB[fi]], [bankB[6]], partial=True,
                              out=acc[3][:, 0, o0:o0 + 2], lhsT=kp[:, w * 128:(w + 1) * 128],
                              rhs=q_tmp[:, fi, 0:2], start=True, stop=True)
                    for s_ in range(2):
                        eb = EB[par * 2 + s_]
                        Et = E_all[:, par, s_, :]
                        O("act", "activation", [bankB[2 * s_], bankB[2 * s_ + 1]], [eb], partial=True,
                          out=split2(Et[:, 0:1024]), in_=acc[s_][:, 0:2, :], func=AF.Exp, scale=0.125)
                        O("act", "activation", [bankB[6]], [eb], partial=True, out=Et[:, 1024:1028],
                          in_=acc[3][:, 0, s_ * 8:s_ * 8 + 4], func=AF.Exp, scale=0.125)
                        O("dve", "tensor_tensor", [eb, maskpB], [eb], out=Et[:, 0:1028], in0=Et[:, 0:1028],
                          in1=maskp[:, 0:1028], op=ALU.mult)

                def att_PV(fi, g=g, j=j):
                    c = 4 * g + fi
                    par = fi % 2
                    for (hb, e0, kind) in ((0, 0, "v"), (1, 8, "d")):
                        for b in range(4):
                            n = 0
                            for s_ in range(2):
                                for w in range(2):
                                    kt = b + 1 + w
                                    reg = (b * 2 + w) * 128
                                    lt = vpad[:, kt, j, s_, :] if kind == "v" else ones_s[:, s_, :]
                                    rb = [vpB[kt]] if kind == "v" else [constB]
                                    O("pe", "matmul", rb + [EB[par * 2 + s_]], [bankB[4 + hb]], partial=True,
                                      out=acc[2][:, hb, b * 128:(b + 1) * 128], lhsT=lt,
                                      rhs=E_all[:, par, s_, reg:reg + 128], start=(n == 0), stop=(n == 3))
                                    n += 1
                        n = 0
                        for s_ in range(2):
                            for w in range(2):
                                lt = vpad[:, w, j, s_, :] if kind == "v" else ones_s[:, s_, :]
                                rb = [vpB[w]] if kind == "v" else [constB]
                                O("pe", "matmul", rb + [EB[par * 2 + s_]], [bankB[7]], partial=True,
                                  out=acc[3][:, 1, e0:e0 + 2], lhsT=lt,
                                  rhs=E_all[:, par, s_, 1024 + w * 2:1024 + w * 2 + 2],
                                  start=(n == 0), stop=(n == 3))
                                n += 1
                    O("dve", "tensor_scalar", [bankB[5], constB], [rtB], partial=True, out=rt_t[:, 2:NE],
                      in0=acc[2][:, 1, 0:512], scalar1=esink[:, c:c + 1], scalar2=None, op0=ALU.add)
                    O("dve", "reciprocal", [rtB], [rtB], partial=True, out=rt_t[:, 2:NE], in_=rt_t[:, 2:NE])
                    O("dve", "tensor_tensor", [bankB[4], rtB], [atB[c]], partial=True, out=attn[:, c, 2:NE],
                      in0=acc[2][:, 0, 0:512], in1=rt_t[:, 2:NE], op=ALU.mult)
                    O("dve", "tensor_scalar", [bankB[7], constB], [rtB], partial=True, out=rt_t[:, 0:2],
                      in0=acc[3][:, 1, 8:10], scalar1=esink[:, c:c + 1], scalar2=None, op0=ALU.add)
                    O("dve", "reciprocal", [rtB], [rtB], partial=True, out=rt_t[:, 0:2], in_=rt_t[:, 0:2])
                    O("dve", "tensor_tensor", [bankB[7], rtB], [atB[c]], partial=True, out=attn[:, c, 0:2],
                      in0=acc[3][:, 1, 0:2], in1=rt_t[:, 0:2], op=ALU.mult)

                def att_all(att_S=att_S, att_PV=att_PV):
                    inherit(bankB, accB)
                    att_S(0)
                    for fi in range(4):
                        if fi + 1 < 4:
                            att_S(fi + 1)
                        att_PV(fi)
                    inherit(accB, bankB)

                code(att_all)
            dbg("attn", attn[:].rearrange("p c n -> p (c n)"), [128, 16 * NE], BF16, atB)

            inherit([uB], [mgtmpB])
            inherit(mixB, [mgtmpB])

            def pool_u(gi):
                wnd = (2, 4, 8, 16)[gi]
                mx = mixed2[gi % 2]
                mb = mixB[gi % 2]

                def rhs_u(k, h):
                    return h0[:, k, 128 + h * 320:128 + (h + 1) * 320]

                def evac_u(f, fi, ai):
                    O("act", "activation", [accB[ai]], [uB], out=split2(u_t), in_=acc[ai][:, 0:2, 0:320],
                      func=AF.Copy)
                    cur = u_t
                    cb = uB
                    step = 1
                    i = 0
                    while step < wnd:
                        dst = s_t[i % 2]
                        O("dve", "tensor_tensor", [cb, mgtmpB], [mgtmpB], partial=True, out=dst[:, step:640],
                          in0=cur[:, step:640], in1=cur[:, 0:640 - step], op=ALU.add)
                        cur = dst
                        cb = mgtmpB
                        step *= 2
                        i += 1
                    O("dve", "tensor_tensor", [mgtmpB, uB, spB], [mgtmpB], partial=True, out=tmp_p,
                      in0=cur[:, 126:640], in1=invc[:, gi, :], op=ALU.mult)
                    O("dve", "tensor_tensor", [mgtmpB, uB], [mb], partial=True, out=mx[:, fi, :],
                      in0=tmp_p, in1=u_t[:, 126:640], op=ALU.subtract)

                proj_fm(w_in, NCH, COL_U + gi * 512, 4, rhs_u, lambda k: [h0B[k]], [320, 320], evac_u)

            def pool_w(gi):
                mx = mixed2[gi % 2]
                mb = mixB[gi % 2]

                def evac_pool(f, fi, ai):
                    c = gi * 4 + fi
                    O("act", "activation", [accB[ai], constB], [poB[c]], out=split2(pool_o[:, c, :]),
                      in_=acc[ai][:, 0:2, 0:HN], func=AF.Copy, scale=pscale[:, c:c + 1])

                proj_fm(w_pool[gi], 4, 0, 4, lambda k, h: mx[:, k, h * HN:(h + 1) * HN], lambda k: [mb],
                        [HN, HN], evac_pool)

            pool_u(0)
            for gi in range(4):
                if gi + 1 < 4:
                    pool_u(gi + 1)
                pool_w(gi)
            dbg("pool", pool_o[:].rearrange("p c n -> p (c n)"), [128, 16 * NE], BF16, poB)

            inherit(mgB, [mgtmpB])
            inherit(sgaB + sgpB, qB + EB + [rtB, uB, spB])
            for fg in range(8):
                def evac_sig(dst, dB):
                    def f_(f, fi, ai):
                        O("act", "activation", [accB[ai]], [dB[fi]], out=split2(dst[:, fi, :]),
                          in_=acc[ai][:, 0:2, 0:HN], func=AF.Sigmoid)
                    return f_

                proj_fm(w_in, NCH, COL_GA + fg * 512, 4, rhs_h0e, lambda k: [h0B[k]], [HN, HN],
                        evac_sig(sga, sgaB))
                proj_fm(w_in, NCH, COL_GP + fg * 512, 4, rhs_h0e, lambda k: [h0B[k]], [HN, HN],
                        evac_sig(sgp, sgpB))

                def evac_ya(f, fi, ai):
                    O("dve", "tensor_tensor", [accB[ai], sgaB[fi]], [sgaB[fi]], out=split2(sga[:, fi, :]),
                      in0=split2(sga[:, fi, :]), in1=acc[ai][:, 0:2, 0:HN], op=ALU.mult)

                def evac_yp(f, fi, ai, fg=fg):
                    O("dve", "tensor_tensor", [accB[ai], sgpB[fi]], [sgpB[fi]], out=split2(sgp[:, fi, :]),
                      in0=split2(sgp[:, fi, :]), in1=acc[ai][:, 0:2, 0:HN], op=ALU.mult)
                    cm = fg * 4 + fi
                    O("dve", "tensor_tensor", [sgaB[fi], sgpB[fi]], [mgB[cm]], out=merged[:, cm, :], in0=sga[:, fi, :],
                      in1=sgp[:, fi, :], op=ALU.add)

                proj_fm(w_ba, 16, fg * 512, 4, lambda k, h: attn[:, k, h * HN:(h + 1) * HN], lambda k: [atB[k]],
                        [HN, HN], evac_ya)
                proj_fm(w_bp, 16, fg * 512, 4, lambda k, h: pool_o[:, k, h * HN:(h + 1) * HN], lambda k: [poB[k]],
                        [HN, HN], evac_yp)
            dbg("merged", merged[:].rearrange("p c n -> p (c n)"), [128, NCH * NE], BF16, mgB)

            inherit(R1B + [xrB], h0B + kpB + vpB + atB + poB + [spB])
            inherit(xsrB, [xrB])
            for t in (1, 2):
                O("sp", "dma_start", [], [xsrB[t % 2]], dma=f"xs{t % 2}", out=xs_r[t % 2], in_=x_rows(ps, t))
            proj_fm(w_out, NCH, 0, NCH, lambda k, h: merged[:, k, h * HN:(h + 1) * HN], lambda k: [mgB[k]],
                    [HN, HN], evac_copy(R1, R1B, True))
            dbg("mix", R1f, [128, NCH * NE], F32, R1B)
            post_scale(R1, R1B, 1)
            ai_n = next_acc()
            acc_reserved.add(ai_n)
            for t in (1, 2, 3, 4, 5):
                xr_ = xs_r[t % 2]
                xrb_ = xsrB[t % 2]
                for c0 in range(0, NCH, 8):
                    ai = next_acc()
                    for cc in range(8):
                        c = c0 + cc
                        O("pe", "transpose", [xrb_, constB], [accB[ai]], partial=True,
                          out=acc[ai][:, cc // 4, (cc % 4) * 128:(cc % 4 + 1) * 128],
                          in_=xr_[:, c * 128:(c + 1) * 128], identity=ident[:])
                    if c0 == NCH - 8 and t + 2 <= 5:
                        O("sp", "dma_start", [], [xrb_], dma=f"xs{t % 2}", out=xr_, in_=x_rows(ps, t + 2))
                    for hb in range(2):
                        cb = c0 + hb * 4
                        srcv = acc[ai][:, hb, :].rearrange("p (c n) -> p c n", c=4)
                        if t == 1:
                            dst = R1[:, cb:cb + 4, 0:2]
                            src = srcv[:, :, 126:128]
                        else:
                            dst = R1[:, cb:cb + 4, 2 + (t - 2) * 128:2 + (t - 1) * 128]
                            src = srcv
                        O("dve", "tensor_tensor", [accB[ai]], R1B[cb:cb + 4], partial=True, out=dst, in0=dst,
                          in1=src, op=ALU.add)
                        if t == 5:
                            for c in range(cb, cb + 4):
                                sq_add(R1, R1B, c, ai_n, c == 0, c == NCH - 1)
            dbg("x1", R1f, [128, NCH * NE], F32, R1B)

            inherit(h1B, mgB + poB + atB + [xrB, spB] + xsrB)
            rstd_finish(ai_n)
            acc_reserved.discard(ai_n)
            h_scale(R1, R1B, 2, h1, h1B)
            O("sp", "dma_start", R1B, [scrB], dma="scr", out=scr_d, in_=R1f)

            inherit(aB, R1B + h0B + kpB + vpB + atB + poB + [xrB] + xsrB)
            inherit([ffB], mgB + [spB])

            def rhs_h1(k, h):
                return h1[:, k, h * 256:h * 256 + 258]

            def conv_copy(ai, fi):
                O("act", "activation", [accB[ai]], [upcB[fi]], out=upc[fi], in_=acc[ai][:, 0:2, 0:258],
                  func=AF.Copy)

            def conv_taps(ch, dst, dstB, fi):
                dv = split2(dst)
                uc = upc[fi]
                ub = upcB[fi]
                O("dve", "tensor_scalar", [ub, constB], [dstB], out=dv, in0=uc[:, :, 2:258],
                  scalar1=convp[:, ch, 2:3], scalar2=convp[:, ch, 3:4], op0=ALU.mult, op1=ALU.add)
                O("dve", "scalar_tensor_tensor", [ub, constB, dstB], [dstB], out=dv,
                  in0=uc[:, :, 1:257], scalar=convp[:, ch, 1:2], in1=dv, op0=ALU.mult, op1=ALU.add)
                O("dve", "scalar_tensor_tensor", [ub, constB, dstB], [dstB], out=dv,
                  in0=uc[:, :, 0:256], scalar=convp[:, ch, 0:1], in1=dv, op0=ALU.mult, op1=ALU.add)

            inherit(cvgB + cvvB + gactB + upcB, [ffB])
            cast_pol[0] = ["act"]
            for i0 in range(0, NFF, 4):
                ng = min(4, NFF - i0)

                def evac_gate(f, fi, ai, i0=i0):
                    ch = i0 + fi
                    conv_copy(ai, fi)

                    def post():
                        conv_taps(ch, cvg[fi], cvgB[fi], fi)
                        defer(lambda: O("act", "activation", [cvgB[fi]], [gactB[fi]], out=gact[:, fi, :],
                                        in_=cvg[fi], func=AF.Gelu_apprx_tanh), 8)
                    return post

                def evac_val(f, fi, ai, i0=i0):
                    ch = NFF + i0 + fi
                    q = fi % 2
                    conv_copy(ai, fi)

                    def post():
                        conv_taps(ch, cvv[q], cvvB[q], fi)
                        O("dve", "tensor_tensor", [cvvB[q], gactB[fi]], [aB[i0 + fi]], out=a_t[:, i0 + fi, :],
                          in0=cvv[q], in1=gact[:, fi, :], op=ALU.mult)
                    return post

                proj_fm(w_up, NCH, i0 * 128, ng, rhs_h1, lambda k: [h1B[k]], [258, 258], evac_gate)
                proj_fm(w_up, NCH, DFF + i0 * 128, ng, rhs_h1, lambda k: [h1B[k]], [258, 258], evac_val)
            dbg("a", a_t[:].rearrange("p c n -> p (c n)"), [128, NFF * TP], BF16, aB)
            cast_pol[0] = ["act", "dve"]

            inherit(R2B, h1B + mgB + poB + [ffB, spB] + TMPB)
            inherit([xrB], [ffB] + h1B + atB + poB + TMPB)
            for c in range(NCH):
                O("pool", "memset", [], [R2B[c]], partial=True, ap=R2[:, c, 0:2], constant=0.0)
            proj_fm(w_down, NFF, 0, NCH, lambda k, h: a_t[:, k, h * 256:(h + 1) * 256], lambda k: [aB[k]],
                    [256, 256], evac_copy(R2, R2B, False))
            flush()
            O("sp", "dma_start", [scrB], [xrB], dma="xr", out=xr_flat, in_=scr_d[:, 0:4 * NE])
            fm_rstd(R2, R2B)
            ai_n = next_acc()
            acc_reserved.add(ai_n)
            for c0 in range(0, NCH, 4):
                for c in range(c0, c0 + 4):
                    O("dve", "scalar_tensor_tensor", [rstdB, constB], [R2B[c]], out=R2[:, c, :], in0=R2[:, c, :],
                      scalar=gains[:, 3, c:c + 1], in1=rstd_b[:], op0=ALU.mult, op1=ALU.mult)
                O("dve", "tensor_tensor", [xrB], R2B[c0:c0 + 4], partial=True, out=R2[:, c0:c0 + 4, :],
                  in0=R2[:, c0:c0 + 4, :], in1=xr_t[:], op=ALU.add)
                if c0 + 4 < NCH:
                    O("sp", "dma_start", [scrB], [xrB], dma="xr", out=xr_flat,
                      in_=scr_d[:, (c0 + 4) * NE:(c0 + 8) * NE])
                for c in range(c0, c0 + 4):
                    sq_add(R2, R2B, c, ai_n, c == 0, c == NCH - 1)
            dbg("x2", abytes(O_R2, SZ_R, F32), [128, NCH * NE], F32, R2B)

            inherit(h2B + R3B, aB + [xrB] + R1B + h0B + kpB + vpB + atB + poB)
            rstd_finish(ai_n)
            acc_reserved.discard(ai_n)
            h_scale(R2, R2B, 4, h2, h2B)
            for t in range(4):
                O("sp", "dma_start", [], [pstageB], dma="pst", out=pstage[:],
                  in_=p_in[ps * TP + t * 128:ps * TP + (t + 1) * 128, :])
                ai = next_acc()
                for kc in range(2):
                    O("pe", "transpose", [pstageB, constB], [accB[ai]], partial=True,
                      out=acc[ai][:, 0, kc * 128:(kc + 1) * 128], in_=pstage[:, kc * 128:(kc + 1) * 128],
                      identity=ident[:])
                O("act", "activation", [accB[ai]], [pTB], partial=True, out=pT[:, :, t * 128:(t + 1) * 128],
                  in_=acc[ai][:, 0, 0:256].rearrange("p (k n) -> p k n", k=2), func=AF.Copy)
            for c in range(NCH):
                O("pool", "memset", [], [R3B[c]], partial=True, ap=R3[:, c, 0:2], constant=0.0)
            for fg in range(8):
                def evac_pg(f, fi, ai):
                    O("act", "activation", [accB[ai]], [sgpleB], partial=True, out=split2(sgple[:, fi, :]),
                      in_=acc[ai][:, 0:2, 0:256], func=AF.Sigmoid)

                def evac_ple(f, fi, ai, fg=fg):
                    cm = fg * 4 + fi
                    O("dve", "tensor_tensor", [accB[ai], sgpleB], [R3B[cm]], partial=True,
                      out=split2(R3[:, cm, 2:NE]), in0=split2(sgple[:, fi, :]), in1=acc[ai][:, 0:2, 0:256],
                      op=ALU.mult)

                proj_fm(w_pg, NCH, fg * 512, 4, lambda k, h: h2[:, k, 2 + h * 256:2 + (h + 1) * 256],
                        lambda k: [h2B[k]], [256, 256], evac_pg)
                proj_fm(w_ple, 2, fg * 512, 4, lambda k, h: pT[:, k, h * 256:(h + 1) * 256], lambda k: [pTB],
                        [256, 256], evac_ple)
            flush()
            post_scale(R3, R3B, 5)
            for c0 in range(0, NCH, 4):
                O("dve", "tensor_tensor", R2B[c0:c0 + 4], R3B[c0:c0 + 4], partial=True, out=R3[:, c0:c0 + 4, :],
                  in0=R3[:, c0:c0 + 4, :], in1=R2[:, c0:c0 + 4, :], op=ALU.add)

            inherit(ostB2, h2B + [ostB])
            for t in range(4):
                ost_ = ostage[t % 2]
                ostb_ = ostB2[t % 2]
                for c0 in range(0, NCH, 8):
                    ai = next_acc()
                    for cc in range(8):
                        c = c0 + cc
                        O("pe", "transpose", [R3B[c], constB], [accB[ai]], partial=True,
                          out=acc[ai][:, cc // 4, (cc % 4) * 128:(cc % 4 + 1) * 128],
                          in_=R3[:, c, 2 + t * 128:2 + (t + 1) * 128], identity=ident[:])
                    if (c0 // 8) % 2 == 0:
                        O("act", "activation", [accB[ai]], [ostb_], partial=True,
                          out=split2(ost_[:, c0 * 128:(c0 + 8) * 128]), in_=acc[ai][:, 0:2, :], func=AF.Copy)
                    else:
                        O("dve", "tensor_copy", [accB[ai]], [ostb_], partial=True,
                          out=split2(ost_[:, c0 * 128:(c0 + 8) * 128]), in_=acc[ai][:, 0:2, :])
                O("sp", "dma_start", [ostb_], [outB], partial=True, dma="out",
                  out=out_d[ps * TP + t * 128:ps * TP + (t + 1) * 128, :], in_=ost_)

        try:
            for ps_ in range(n_pass):
                run_pass(ps_)
        except _Stop:
            pass
        S.emit(final_waits=[("sp", outB), ("sp", dbgB)])
    return nc, dbg_out


def _consts():
    kj = np.arange(128)[:, None]
    qi = np.arange(128)[None, :]
    prev = (kj > qi).astype(np.float32)
    same = (kj <= qi).astype(np.float32)
    mask = np.zeros((128, 1032), np.float32)
    for b in range(4):
        mask[:, (b * 2) * 128:(b * 2 + 1) * 128] = prev
        mask[:, (b * 2 + 1) * 128:(b * 2 + 2) * 128] = same
    mask[:, 1024:1026] = prev[:, 126:128]
    mask[:, 1026:1028] = same[:, 126:128]
    import ml_dtypes
    return mask.astype(ml_dtypes.bfloat16), np.eye(128, dtype=np.float32)


def make_in_maps(inp, cores=range(N_CORES)):
    f = lambda a: np.ascontiguousarray(np.asarray(a, dtype=np.float32))
    x = f(inp["x"])[0]
    p = f(inp["p"])[0, 0]
    gl = np.stack([f(inp[k])[0] for k in ("norm_mix_pre", "norm_mix_post", "norm_ffn_pre", "norm_ffn_post",
                                           "norm_ple_gate", "norm_ple_post")])
    gains = np.ascontiguousarray(gl.reshape(6, NCH, 128).transpose(2, 0, 1)).reshape(128, 6 * NCH)
    cw = f(inp["conv_w"])[0]
    cb = f(inp["conv_b"])[0]
    cp = np.concatenate([cw, cb[None]], 0)
    convp = np.ascontiguousarray(cp.reshape(4, 172, 128).transpose(2, 1, 0)).reshape(128, 172 * 4)
    pscale = np.ascontiguousarray(f(inp["pool_scale"])[0].reshape(16, 128).T)
    sk = f(inp["attn_sinks"])[0]
    sinks = np.ascontiguousarray(np.repeat(sk.reshape(16, 2).T, 64, axis=0))
    mask, ident = _consts()
    shared = dict(
        w_in=f(inp["w_in"])[0], w_pool=f(inp["w_pool"])[0], w_ba=f(inp["w_branch_attn"])[0],
        w_bp=f(inp["w_branch_pool"])[0], w_out=f(inp["w_out"])[0], w_up=f(inp["w_up"])[0],
        w_down=f(inp["w_down"])[0], w_pg=f(inp["w_ple_gate"])[0], w_ple=f(inp["w_ple"])[0],
        gains=gains, convp=convp, pscale=pscale, sinks=sinks, mask=mask, ident=ident,
        g0b=np.ascontiguousarray(np.broadcast_to(gl[0][None, :], (128, D))))
    xpad = np.concatenate([np.zeros((256, D), np.float32), x], 0)
    maps = []
    for c in cores:
        m = dict(shared)
        m["xh"] = np.ascontiguousarray(xpad[c * TOK_CORE:c * TOK_CORE + TOK_CORE + 256])
        m["p"] = np.ascontiguousarray(p[c * TOK_CORE:(c + 1) * TOK_CORE])
        invc = np.zeros((2, 4, NE), np.float32)
        hvv = np.zeros((128, 2), np.float32)
        for ps in range(2):
            t0 = c * TOK_CORE + ps * TP
            pos = np.arange(t0 - 2, t0 + TP)
            for gi, w in enumerate((2, 4, 8, 16)):
                invc[ps, gi] = 1.0 / np.clip(np.minimum(pos + 1, w), 1, None)
            hvv[:, ps] = 1.0 if t0 > 0 else 0.0
        m["invc"] = np.ascontiguousarray(np.broadcast_to(invc.reshape(1, -1), (128, 2 * 4 * NE)))
        m["hv"] = hvv
        maps.append(m)
    return maps


_NC_CACHE = {}


def kernel(**inputs):
    if "nc" not in _NC_CACHE:
        _NC_CACHE["nc"] = build_nc()[0]
    nc = _NC_CACHE["nc"]
    in_maps = make_in_maps(inputs)
    res = run_bass_kernel_spmd(nc, in_maps, core_ids=list(range(N_CORES)))
    out = np.concatenate([np.asarray(r["out"], dtype=np.float32) for r in res.results], axis=0)
    return out.reshape(1, N_CORES * TOK_CORE, D)
```
